# Optimizing a Trainium2 kernel written in Bass

```python
import jax, jax.numpy as jnp
from jax import lax
import numpy as np

D_MODEL = 2048
BATCH = 2
SEQ = 4096
DEPTH = 1

D_POOL = D_MODEL // 2
N_POOL_GROUPS = 4
POOL_GROUP = D_POOL // N_POOL_GROUPS
POOL_WINDOWS = (2, 4, 8, 16)
D_CONV = D_MODEL // 2
CONV_WIDTH = 31
D_FF = 5632
N_BRANCHES = 2
D_IN = D_POOL + 2 * D_CONV + N_BRANCHES * D_MODEL
EPS = 1e-6

kernel_name = "macaron_gated_pool_conv_encoder"


def rmsnorm(x, g):
    xf = x.astype(jnp.float32)
    y = xf * lax.rsqrt(jnp.mean(xf * xf, axis=-1, keepdims=True) + EPS)
    return (y * g.astype(jnp.float32)).astype(x.dtype)


def layernorm(x, g, b):
    xf = x.astype(jnp.float32)
    mu = jnp.mean(xf, axis=-1, keepdims=True)
    var = jnp.mean(jnp.square(xf - mu), axis=-1, keepdims=True)
    y = (xf - mu) * lax.rsqrt(var + EPS)
    return (y * g.astype(jnp.float32) + b.astype(jnp.float32)).astype(x.dtype)


def swiglu(h, w_gate, w_up, w_down):
    return (jax.nn.silu(h @ w_gate) * (h @ w_up)) @ w_down


def centred_window_mean(u, w):
    T = u.shape[1]
    cs = jnp.cumsum(u.astype(jnp.float32), axis=1)
    cs = jnp.pad(cs, ((0, 0), (1, 0), (0, 0)))
    t = jnp.arange(T)
    lo = jnp.clip(t - w // 2, 0, T)
    hi = jnp.clip(t + (w - w // 2), 0, T)
    count = (hi - lo).astype(jnp.float32)
    s = jnp.take(cs, hi, axis=1) - jnp.take(cs, lo, axis=1)
    return (s / count[None, :, None]).astype(u.dtype)


def pool_branch(u, w_group, scale, w_proj):
    B, T, _ = u.shape
    ug = u.reshape(B, T, N_POOL_GROUPS, POOL_GROUP)
    pooled = jnp.stack(
        [centred_window_mean(ug[:, :, gi], w) - ug[:, :, gi] for gi, w in enumerate(POOL_WINDOWS)],
        axis=2)
    mixed = jnp.einsum('btgc,gcd->btgd', pooled, w_group)
    mixed = mixed.reshape(B, T, D_POOL) * scale
    return mixed @ w_proj


def conv_branch(v, dw_w, dw_b, ln_g, ln_b, w_proj, b_proj):
    v1, v2 = jnp.split(v, 2, axis=-1)
    glu = v1 * jax.nn.sigmoid(v2)
    conv = lax.conv_general_dilated(
        glu, dw_w.reshape(CONV_WIDTH, 1, D_CONV).astype(glu.dtype),
        window_strides=(1,), padding='SAME',
        dimension_numbers=('NWC', 'WIO', 'NWC'),
        feature_group_count=D_CONV) + dw_b
    y = jax.nn.silu(layernorm(conv, ln_g, ln_b))
    return y @ w_proj + b_proj


def setup_inputs(seed: int = 0) -> dict:
    key = jax.random.key(seed)
    ks = iter(jax.random.split(key, 32))
    f32 = jnp.float32

    def nrm(shape, fan_in):
        return jax.random.normal(next(ks), shape, f32) * (fan_in ** -0.5)

    def gain(shape):
        return 1.0 + 0.02 * jax.random.normal(next(ks), shape, f32)

    def bias(shape):
        return 0.02 * jax.random.normal(next(ks), shape, f32)

    L = DEPTH
    return {
        "x": jax.random.normal(next(ks), (BATCH, SEQ, D_MODEL), f32),
        "ffn1_norm": gain((L, D_MODEL)),
        "ffn1_w_gate": nrm((L, D_MODEL, D_FF), D_MODEL),
        "ffn1_w_up": nrm((L, D_MODEL, D_FF), D_MODEL),
        "ffn1_w_down": nrm((L, D_FF, D_MODEL), D_FF),
        "mix_norm": gain((L, D_MODEL)),
        "w_in": nrm((L, D_MODEL, D_IN), D_MODEL),
        "b_gate": bias((L, N_BRANCHES * D_MODEL)),
        "pool_w_group": nrm((L, N_POOL_GROUPS, POOL_GROUP, POOL_GROUP), POOL_GROUP),
        "pool_scale": gain((L, D_POOL)),
        "pool_w_proj": nrm((L, D_POOL, D_MODEL), D_POOL),
        "conv_dw_w": nrm((L, CONV_WIDTH, D_CONV), CONV_WIDTH),
        "conv_dw_b": bias((L, D_CONV)),
        "conv_ln_g": gain((L, D_CONV)),
        "conv_ln_b": bias((L, D_CONV)),
        "conv_w_proj": nrm((L, D_CONV, D_MODEL), D_CONV),
        "conv_b_proj": bias((L, D_MODEL)),
        "w_out": nrm((L, D_MODEL, D_MODEL), D_MODEL),
        "ffn2_norm": gain((L, D_MODEL)),
        "ffn2_w_gate": nrm((L, D_MODEL, D_FF), D_MODEL),
        "ffn2_w_up": nrm((L, D_MODEL, D_FF), D_MODEL),
        "ffn2_w_down": nrm((L, D_FF, D_MODEL), D_FF),
        "final_norm": gain((D_MODEL,)),
    }


def reference(x, ffn1_norm, ffn1_w_gate, ffn1_w_up, ffn1_w_down, mix_norm, w_in, b_gate,
              pool_w_group, pool_scale, pool_w_proj, conv_dw_w, conv_dw_b, conv_ln_g,
              conv_ln_b, conv_w_proj, conv_b_proj, w_out, ffn2_norm, ffn2_w_gate,
              ffn2_w_up, ffn2_w_down, final_norm):
    for l in range(DEPTH):
        x = x + 0.5 * swiglu(rmsnorm(x, ffn1_norm[l]), ffn1_w_gate[l], ffn1_w_up[l], ffn1_w_down[l])
        h = rmsnorm(x, mix_norm[l])
        z = h @ w_in[l]
        u = z[..., :D_POOL]
        v = z[..., D_POOL:D_POOL + 2 * D_CONV]
        g_logits = z[..., D_POOL + 2 * D_CONV:] + b_gate[l]
        a = pool_branch(u, pool_w_group[l], pool_scale[l], pool_w_proj[l])
        b = conv_branch(v, conv_dw_w[l], conv_dw_b[l], conv_ln_g[l], conv_ln_b[l],
                        conv_w_proj[l], conv_b_proj[l])
        g_a, g_b = jnp.split(jax.nn.sigmoid(g_logits), N_BRANCHES, axis=-1)
        x = x + (g_a * a + g_b * b) @ w_out[l]
        x = x + 0.5 * swiglu(rmsnorm(x, ffn2_norm[l]), ffn2_w_gate[l], ffn2_w_up[l], ffn2_w_down[l])
    return rmsnorm(x, final_norm)
```

```python
import contextlib
import os
import numpy as np
import concourse.bass as bass
import concourse.mybir as mybir
from concourse.bass_utils import run_bass_kernel_spmd

F32 = mybir.dt.float32
BF16 = mybir.dt.bfloat16
AF = mybir.ActivationFunctionType
ALU = mybir.AluOpType

D = 2048
DFF = 5632
NJ = DFF // 128
SL = 4
NSLAB = NJ // SL
EPS = 1e-6
HALO = 16
NOWN = 1024
NEXT = NOWN + 2 * HALO
HE = 512 + 2 * HALO
WINDOWS = (2, 4, 8, 16)

def T_G(f, j): return f * 132 + 3 * j
def T_U(f, j): return f * 132 + 3 * j + 1
def T_D(f, j): return f * 132 + 3 * j + 2
T_WIN = 264
T_PGRP = 320
T_PPROJ = 321
T_CPROJ = 329
T_WOUT = 337
NT = 353

C_BG, C_PS, C_DWB, C_LNG, C_LNB, C_BP, C_DWW = 0, 32, 40, 48, 56, 64, 80
NCV = 80 + 8 * 31

RG = 6
RD = 8


class Builder:
    def __init__(self, nc, sems):
        self.nc = nc
        self.q = {e: [] for e in ("pe", "act", "dve", "pool", "sp")}
        self.prog = {e: [sems[e], 0] for e in ("pe", "act", "dve")}
        self.trk = {}
        self.last = {e: None for e in ("pe", "act", "dve")}

    def _deps(self, reads, writes):
        w = []
        for k in reads:
            t = self.trk.get(k)
            if t and t["w"] is not None:
                w.append(t["w"])
        for k in writes:
            t = self.trk.get(k)
            if t:
                if t["w"] is not None:
                    w.append(t["w"])
                w.extend(t["r"])
        return w

    def _commit(self, tok, reads, writes):
        for k in reads:
            self.trk.setdefault(k, {"w": None, "r": []})["r"].append(tok)
        for k in writes:
            self.trk[k] = {"w": tok, "r": []}

    def op(self, eng, fn, reads=(), writes=(), extra=(), signal=True):
        waits = self._deps(reads, writes) + [t for t in extra if t is not None]
        tok = None
        if signal:
            p = self.prog[eng]
            p[1] += 1
            tok = (p[0], p[1])
            self.last[eng] = tok
        self.q[eng].append((waits, fn, tok))
        if signal:
            self._commit(tok, reads, writes)
        return tok

    def dma(self, eng, fn, sem, count, reads=(), writes=(), extra=()):
        waits = self._deps(reads, writes) + [t for t in extra if t is not None]
        tok = (sem, count)
        self.q[eng].append((waits, fn, ("dma", sem)))
        self._commit(tok, reads, writes)
        return tok

    def wait_only(self, eng, toks):
        self.q[eng].append(([t for t in toks if t is not None], None, None))

    def replay(self, eng, e):
        seen = {}
        for waits, fn, tok in self.q[eng]:
            for (s, v) in waits:
                key = id(s)
                if seen.get(key, 0) >= v:
                    continue
                seen[key] = v
                e.wait_ge(s, v)
            if fn is None:
                continue
            ins = fn(e)
            if tok is not None:
                if tok[0] == "dma":
                    ins.then_inc(tok[1], 16)
                else:
                    ins.then_inc(tok[0], 1)


def build_program(stage=99):
    nc = bass.Bass("TRN2", target_bir_lowering=False)
    xin = nc.dram_tensor("xin", [NEXT, D], F32, kind="ExternalInput").ap()
    tbase = 264 if os.environ.get('KSKIPF') else 0
    nt_used = int(os.environ['KNT']) if os.environ.get('KNT') else (NT - tbase) if stage >= 2 else (min(132, 12 * int(os.environ.get('KSLABS', 11))) if stage >= 1 else 1)
    wst = nc.dram_tensor("wst", [nt_used, 128, 2048], F32, kind="ExternalInput").ap()
    gains = nc.dram_tensor("gains", [4, D], F32, kind="ExternalInput").ap()
    cvec = nc.dram_tensor("cvec", [128, NCV], F32, kind="ExternalInput").ap()
    rcin = nc.dram_tensor("rcin", [128, 64], F32, kind="ExternalInput").ap()
    idin = nc.dram_tensor("idin", [128, 128], F32, kind="ExternalInput").ap()
    out = nc.dram_tensor("out", [NOWN, D], F32, kind="ExternalOutput").ap()

    with contextlib.ExitStack() as es:
        def sb(name, shape, dt):
            return es.enter_context(nc.sbuf_tensor(name, shape, dt))

        def sem(name):
            return es.enter_context(nc.semaphore(name))

        xres = sb("xres", [128, 8, D], F32)
        ring = sb("ring", [128, RG, 2048], BF16)
        hTa = sb("hTa", [128, 16 * NEXT], BF16)
        gbc = sb("gbc", [128, D], F32)
        xhat = sb("xhat", [128, D], BF16)
        cv = sb("cv", [128, NCV], F32)
        rc = sb("rc", [128, 64], F32)
        identf = sb("identf", [128, 128], F32)
        identb = sb("identb", [128, 128], BF16)
        ones = sb("ones", [128, 128], BF16)
        ssb = sb("ssb", [128, 16], F32)
        rsb = sb("rsb", [128, 16], F32)
        stash = sb("stash", [128, 16, 32], BF16)
        diag = sb("diag", [128, 16, 128], BF16)
        arena = sb("arena", [128, 16384], F32)
        ps = es.enter_context(nc.psum_tensor("ps", [128, 8, 512], F32))

        sems = {"pe": sem("s_pe"), "act": sem("s_act"), "dve": sem("s_dve")}
        s_ring = [sem(f"s_rg{i}") for i in range(RG)]
        s_dring = [sem(f"s_rd{i}") for i in range(RD)]
        s_x = [sem(f"s_x{i}") for i in range(9)]
        s_c = sem("s_const")
        s_gn = sem("s_gain")
        s_out = sem("s_out")

        B = Builder(nc, sems)

        def aview(off_bytes, shape, dt):
            n = int(np.prod(shape[1:]))
            if dt == F32:
                a = arena[:, off_bytes // 4: off_bytes // 4 + n]
            else:
                a = arena[:, off_bytes // 4: off_bytes // 4 + n // 2].bitcast(BF16)
            if len(shape) == 3:
                a = a.rearrange("p (a b) -> p a b", a=shape[1])
            elif len(shape) == 4:
                a = a.rearrange("p (a b c) -> p a b c", a=shape[1], b=shape[2])
            return a

        xh = aview(0, [128, D], F32)
        dring = aview(8192, [128, RD, 2048], BF16)
        actb = aview(40960, [128, 2, SL, NEXT], BF16)
        sgb = aview(57856, [128, 2, 512], F32)
        glu = aview(0, [128, 8, HE], BF16)
        ybf = aview(0, [128, 8, 512], BF16)
        convb = aview(8704, [128, 8, 512], F32)
        mbf = aview(8704, [128, 16, 512], BF16)
        ubuf = aview(25088, [128, 2, HE], F32)
        ptb = aview(29440, [128, 2, HE], F32)
        sgm = aview(33792, [128, 2, 512], F32)
        sqb = aview(37888, [128, 1, 512], F32)
        hlb = aview(60480, [128, 4, 512], BF16)
        lnm = aview(41984, [128, 3, 512], F32)
        gt = aview(48128, [128, 4, 512], F32)
        ltb = aview(56320, [128, 2, 512], F32)
        tmp8 = aview(60416, [128, 16], F32)
        xhat2 = aview(45056, [128, D], BF16)
        junkA = aview(40960, [128, D], BF16)

        hT = hTa[:, :].rearrange("p (k t) -> p k t", k=16)
        hTm = hTa[:, 0:16 * HE].rearrange("p (k t) -> p k t", k=16)
        pooled = hTa[:, 16 * HE:16 * HE + 8 * 512].rearrange("p (k t) -> p k t", k=8)
        mixedT = hTa[:, 16 * HE + 8 * 512:16 * HE + 16 * 512].rearrange("p (k t) -> p k t", k=8)

        def psb(b, n=512):
            return ps[:, b, 0:n]

        def psT(b):
            return ps[:, b, :].bitcast(BF16).rearrange("p (k t) -> p k t", k=8)

        gstate = {"n": 0, "cnt": [0] * RG}
        dstate = {"n": 0, "cnt": [0] * RD}

        def load_tile(tid, dring_=False, extra=()):
            st, sems_, buf, R, nm = ((dstate, s_dring, dring, RD, "D") if dring_
                                     else (gstate, s_ring, ring, RG, "G"))
            slot = st["n"] % R
            st["n"] += 1
            st["cnt"][slot] += 1
            B.dma("pool", lambda e, s=slot, t=tid, b=buf: e.dma_start(out=b[:, s, :], in_=wst[t - tbase]),
                  sems_[slot], 16 * st["cnt"][slot], writes=[(nm, slot)], extra=extra)
            return slot

        B.dma("sp", lambda e: e.dma_start(out=cv[:, :], in_=cvec), s_c, 48, writes=["cv"])
        B.dma("sp", lambda e: e.dma_start(out=rc[:, :], in_=rcin), s_c, 48, writes=["rc"])
        B.dma("sp", lambda e: e.dma_start(out=identf[:, :], in_=idin), s_c, 48, writes=["identf"])
        B.op("dve", lambda e: e.tensor_copy(out=identb[:, :], in_=identf[:, :]), reads=["identf"], writes=["identb"])
        B.op("dve", lambda e: e.memset(ones[:, :], 1.0), writes=["ones"])
        gain_n = [0]

        def load_gain(i):
            gain_n[0] += 1
            B.dma("sp", lambda e, i=i: e.dma_start(out=gbc[:, :], in_=gains[i:i + 1, :].broadcast_to([128, D])),
                  s_gn, 16 * gain_n[0], writes=["gbc"])

        load_gain(0)
        for c in range(8):
            B.dma("sp", lambda e, c=c: e.dma_start(out=xres[:, c, :], in_=xin[c * 128:(c + 1) * 128, :]),
                  s_x[c], 16, writes=[("x", c)])
        B.dma("sp", lambda e: e.dma_start(out=xh[0:32, :], in_=xin[1024:1056, :]), s_x[8], 16, writes=[("x", 8)])

        def xchunk(c, P=128):
            return xh[0:32, :] if c == 8 else xres[:, c, :]

        def rstd_chunk(c):
            P = 32 if c == 8 else 128
            x = xchunk(c)
            B.op("act", lambda e: e.activation(out=junkA[0:P, :], in_=x, func=AF.Square,
                                               accum_out=ssb[0:P, c:c + 1]),
                 reads=[("x", c)], writes=["junkA", ("ss", c)])
            B.op("dve", lambda e: e.tensor_scalar(out=rsb[0:P, c:c + 1], in0=ssb[0:P, c:c + 1],
                                                  scalar1=1.0 / D, scalar2=EPS, op0=ALU.mult, op1=ALU.add),
                 reads=[("ss", c)], writes=[("rs", c)])
            B.op("act", lambda e: e.activation(out=ssb[0:P, c:c + 1], in_=rsb[0:P, c:c + 1], func=AF.Sqrt),
                 reads=[("rs", c)], writes=[("ss", c)])
            B.op("dve", lambda e: e.reciprocal(out=rsb[0:P, c:c + 1], in_=ssb[0:P, c:c + 1]),
                 reads=[("ss", c)], writes=[("rs", c)])

        tbank = [0]
        pchunk = [0]

        def prep_seq(items):
            for (c, dests, cr) in items:
                if cr:
                    rstd_chunk(c)
            pend = []
            for (c, dests, cr) in items:
                ev = prep_xt(c, dests)
                for f in pend:
                    f()
                pend = ev
            for f in pend:
                f()

        def prep_xt(c, dests):
            P = 32 if c == 8 else 128
            evs = []
            x = xchunk(c)
            xi = pchunk[0] % 2
            pchunk[0] += 1
            xhat_ = xhat if xi == 0 else xhat2
            xkey = ("xhat", xi)
            B.op("dve", lambda e: e.scalar_tensor_tensor(out=xhat_[0:P, :], in0=x, scalar=rsb[0:P, c:c + 1],
                                                         in1=gbc[0:P, :], op0=ALU.mult, op1=ALU.mult),
                 reads=[("x", c), ("rs", c), "gbc"], writes=[xkey])
            for q in range(4):
                b = tbank[0] % 8
                tbank[0] += 1
                pt = ps[:, b, :].rearrange("p (k t) -> p k t", k=4)
                for k in range(4):
                    kk = q * 4 + k
                    B.op("pe", lambda e, k=k, kk=kk, pt=pt: e.matmul(pt[:, k, 0:P], xhat_[0:P, kk * 128:(kk + 1) * 128],
                                                                     identb[0:P, 0:P], start=True, stop=True),
                         reads=[xkey, "identb"] if k == 0 else [], writes=[("ps", b)] if k == 0 else [],
                         signal=(k == 3))
                B._commit(B.last["pe"], [xkey], [("ps", b)])
                for di, (dst, lo, hi, keys) in enumerate(dests):
                    if q % 2 == 0:
                        evs.append(lambda dst=dst, lo=lo, hi=hi, pt=pt, q=q, b=b, keys=keys: B.op(
                            "act", lambda e: e.copy(out=dst(q * 4), in_=pt[:, :, lo:hi]),
                            reads=[("ps", b)], writes=[(k_, q) for k_ in keys]))
                    else:
                        evs.append(lambda dst=dst, lo=lo, hi=hi, pt=pt, q=q, b=b, keys=keys: B.op(
                            "dve", lambda e: e.tensor_copy(out=dst(q * 4), in_=pt[:, :, lo:hi]),
                            reads=[("ps", b)], writes=[(k_, q) for k_ in keys]))
            return evs

        def ffn(f, ntok, ttiles, nchunks):
            gu_banks = [0, 1, 2, 3]
            gub = [0]
            sgi = [0]

            def hkeys(s, e):
                ks = []
                for blk in range(9):
                    a, b_ = (blk * 128, blk * 128 + 128) if blk < 8 else (1024, 1056)
                    if a < e and b_ > s and a < ntok:
                        ks += [(("hT", blk), q_) for q_ in range(4)]
                return ks

            def gu_unit(s, jl, j, slot_g, slot_u, ti):
                t0, tn = ttiles[ti]
                bg = gu_banks[gub[0] % 4]
                bu = gu_banks[(gub[0] + 1) % 4]
                gub[0] += 2
                for which, slot, bank in (("g", slot_g, bg), ("u", slot_u, bu)):
                    for kc in range(16):
                        first, last = kc == 0, kc == 15
                        B.op("pe", lambda e, kc=kc, slot=slot, bank=bank, first=first, last=last:
                             e.matmul(psb(bank, tn), ring[:, slot, kc * 128:(kc + 1) * 128],
                                      hT[:, kc, t0:t0 + tn], start=first, stop=last),
                             reads=([("G", slot)] + hkeys(t0, t0 + tn)) if first else [],
                             writes=[("ps", bank)] if first else [], signal=last)
                    B._commit(B.last["pe"], [("G", slot)] + hkeys(t0, t0 + tn), [("ps", bank)])
                si = sgi[0] % 2
                sgi[0] += 1
                B.op("act", lambda e: e.activation(out=sgb[:, si, 0:tn], in_=psb(bg, tn), func=AF.Silu),
                     reads=[("ps", bg)], writes=[("sg", si)])
                B.op("dve", lambda e: e.tensor_tensor(out=actb[:, s % 2, jl, t0:t0 + tn], in0=psb(bu, tn),
                                                      in1=sgb[:, si, 0:tn], op=ALU.mult),
                     reads=[("ps", bu), ("sg", si)], writes=[("act", s % 2, jl, ti)])

            dcnt = [0]

            def down_unit(s, cd, dslots):
                c, dh = cd
                P = 32 if c == 8 else 128
                c0 = 1024 if c == 8 else c * 128
                akeys = [("act", s % 2, jl, ti) for jl in range(SL) for ti, (t0, tn) in enumerate(ttiles)
                         if t0 < c0 + P and t0 + tn > c0]
                dkeys = [("D", sl_) for sl_ in dslots]
                bks = (4, 5) if dcnt[0] % 2 == 0 else (6, 7)
                dcnt[0] += 1
                n = 0
                for jl in range(SL):
                    for d2 in range(2):
                        first, last = jl == 0, (jl == SL - 1 and d2 == 1)
                        dcol = (dh * 2 + d2) * 512
                        B.op("pe", lambda e, jl=jl, d2=d2, first=first, dcol=dcol:
                             e.matmul(ps[0:P, bks[d2], :], actb[:, s % 2, jl, c0:c0 + P],
                                      dring[:, dslots[jl], dcol:dcol + 512],
                                      start=first, stop=(jl == SL - 1)),
                             reads=(akeys + dkeys) if n == 0 else [],
                             writes=[("ps", bks[0]), ("ps", bks[1])] if n == 0 else [], signal=last)
                        n += 1
                B._commit(B.last["pe"], akeys + dkeys, [("ps", bks[0]), ("ps", bks[1])])
                xc = xchunk(c)
                for d2 in range(2):
                    dcol = (dh * 2 + d2) * 512
                    B.op("dve", lambda e, d2=d2, dcol=dcol: e.scalar_tensor_tensor(
                        out=xc[0:P, dcol:dcol + 512], in0=ps[0:P, bks[d2], :], scalar=0.5,
                        in1=xc[0:P, dcol:dcol + 512], op0=ALU.mult, op1=ALU.add),
                        reads=[("ps", bks[d2])], writes=[("x", c)])

            pending_down = []
            nslab = int(os.environ.get('KSLABS', NSLAB))
            for s in range(nslab + 1):
                gus = []
                if s < nslab:
                    for jl in range(SL):
                        for ti in range(len(ttiles)):
                            gus.append((jl, ti))
                downs = list(pending_down)
                pending_down = []
                dslots_prev = getattr(ffn, "_dsl", None)
                ngu, nd = len(gus), len(downs)
                slots = {}
                di = 0
                for gi, (jl, ti) in enumerate(gus):
                    j = s * SL + jl
                    if ti == 0:
                        slots[jl] = (load_tile(T_G(f, j)), load_tile(T_U(f, j)))
                    gu_unit(s, jl, j, slots[jl][0], slots[jl][1], ti)
                    want = ((gi + 1) * nd) // ngu if ngu else nd
                    while di < want:
                        down_unit(s - 1, downs[di], dslots_prev)
                        di += 1
                while di < nd:
                    down_unit(s - 1, downs[di], dslots_prev)
                    di += 1
                if s < nslab:
                    ffn._dsl = [load_tile(T_D(f, s * SL + jl), dring_=True, extra=ffn_dextra) for jl in range(SL)]
                    pending_down = [(c_, dh_) for c_ in range(nchunks) for dh_ in range(2)]

        ffn_dextra = []

        def phase_barrier():
            toks = [B.last["pe"], B.last["act"], B.last["dve"]]
            for eng in ("pe", "act", "dve"):
                B.wait_only(eng, toks)
            return toks

        if stage >= 0.5:
            items = []
            for c in range(9):
                n = 32 if c == 8 else 128
                c0 = 1024 if c == 8 else c * 128
                items.append((c, [(lambda k0, c0=c0, n=n: hT[:, k0:k0 + 4, c0:c0 + n], 0, n, [("hT", c)])], True))
            prep_seq(items)
        if stage >= 1 and not tbase:
            ffn(0, NEXT, [(0, 352), (352, 352), (704, 352)], 9)
        bar = phase_barrier()

        MC = int(os.environ.get('KMIXCUT', 8))

        def mixer(half):
            gc0 = 4 * half
            if MC < 8 and half == 1 and not os.environ.get('KHALF1'):
                return
            rr = [0]
            pool_banks = [0, 1, 2, 3, 4, 5]

            def nb():
                b = pool_banks[rr[0] % 6]
                rr[0] += 1
                return b

            def hmk(s, e):
                ks = []
                blocks = [(0, 16)] + [(16 + 128 * i, 16 + 128 * (i + 1)) for i in range(4)] + [(528, 544)]
                for bi, (a, b_) in enumerate(blocks):
                    if a < e and b_ > s:
                        ks += [(("hm", bi), q_) for q_ in range(4)]
                return ks

            def win_unit(slot, t0, tn, bank):
                for kc in range(16):
                    first, last = kc == 0, kc == 15
                    B.op("pe", lambda e, kc=kc, first=first, last=last:
                         e.matmul(psb(bank, tn), ring[:, slot, kc * 128:(kc + 1) * 128],
                                  hTm[:, kc, t0:t0 + tn], start=first, stop=last),
                         reads=([("G", slot)] + hmk(t0, t0 + tn)) if first else [],
                         writes=[("ps", bank)] if first else [], signal=last)
                B._commit(B.last["pe"], [("G", slot)] + hmk(t0, t0 + tn), [("ps", bank)])

            if half == 0:
                load_gain(1)
                items = [(8, [(lambda k0: hTm[:, k0:k0 + 4, 0:16], 0, 16, [("hm", 0)]),
                              (lambda k0: stash[:, k0:k0 + 4, 16:32], 16, 32, ["stashR"])], True)]
                for c in range(4):
                    dests = [(lambda k0, c=c: hTm[:, k0:k0 + 4, 16 + 128 * c:16 + 128 * (c + 1)], 0, 128, [("hm", 1 + c)])]
                    if c == 3:
                        dests.append((lambda k0: stash[:, k0:k0 + 4, 0:16], 112, 128, ["stashL"]))
                    items.append((c, dests, True))
                items.append((4, [(lambda k0: hTm[:, k0:k0 + 4, 528:544], 0, 16, [("hm", 5)])], True))
                prep_seq(items)
            else:
                for hk in range(2):
                    B.op("dve", lambda e, hk=hk: e.tensor_copy(out=hTm[:, hk * 8:hk * 8 + 8, 0:16],
                                                               in_=stash[:, hk * 8:hk * 8 + 8, 0:16]),
                         reads=[("stashL", 2 * hk), ("stashL", 2 * hk + 1)], writes=[(("hm", 0), 2 * hk), (("hm", 0), 2 * hk + 1)])
                    B.op("dve", lambda e, hk=hk: e.tensor_copy(out=hTm[:, hk * 8:hk * 8 + 8, 528:544],
                                                               in_=stash[:, hk * 8:hk * 8 + 8, 16:32]),
                         reads=[("stashR", 2 * hk), ("stashR", 2 * hk + 1)], writes=[(("hm", 5), 2 * hk), (("hm", 5), 2 * hk + 1)])
                prep_seq([(c, [(lambda k0, c=c: hTm[:, k0:k0 + 4, 16 + 128 * (c - 4):16 + 128 * (c - 3)], 0, 128,
                                [("hm", 1 + c - 4)])], c != 4) for c in range(4, 8)])

            if MC <= 1:
                return
            ET = [(0, 272), (272, 272)]
            sgi = [0]
            dgi = [0]
            conv_tok = {}

            def glu_unit(cc):
                s2 = load_tile(T_WIN + 16 + cc)
                s1 = load_tile(T_WIN + 8 + cc)
                for (t0, tn) in ET:
                    b2, b1 = nb(), nb()
                    win_unit(s2, t0, tn, b2)
                    win_unit(s1, t0, tn, b1)
                    si = sgi[0] % 2
                    sgi[0] += 1
                    B.op("act", lambda e, si=si, b2=b2, tn=tn: e.activation(out=sgm[:, si, 0:tn], in_=psb(b2, tn), func=AF.Sigmoid),
                         reads=[("ps", b2)], writes=[("sgm", si)])
                    B.op("dve", lambda e, si=si, b1=b1, t0=t0, tn=tn: e.tensor_tensor(
                        out=glu[:, cc, t0:t0 + tn], in0=psb(b1, tn), in1=sgm[:, si, 0:tn], op=ALU.mult),
                        reads=[("ps", b1), ("sgm", si)], writes=[("glu", cc, t0)])

            def conv_unit(cc):
                bank = nb()
                gkeys = [("glu", cc, 0), ("glu", cc, 272)]
                for (j0, j1) in ((0, 8), (8, 16), (16, 24), (24, 31)):
                    dbuf = dgi[0] % 2
                    dgi[0] += 1
                    for j in range(j0, j1):
                        col = C_DWW + cc * 31 + j
                        di_ = dbuf * 8 + (j - j0)
                        B.op("dve", lambda e, di_=di_, col=col: e.tensor_scalar(
                            out=diag[:, di_, :], in0=identb[:, :], scalar1=cv[:, col:col + 1], scalar2=None, op0=ALU.mult),
                            reads=["identb", "cv"] if j == j0 else [], writes=[("diag", dbuf)] if j == j0 else [],
                            signal=(j == j1 - 1))
                    B._commit(B.last["dve"], ["identb", "cv"], [("diag", dbuf)])
                    for j in range(j0, j1):
                        di_ = dbuf * 8 + (j - j0)
                        B.op("pe", lambda e, di_=di_, j=j, bank=bank: e.matmul(
                            psb(bank), diag[:, di_, :], glu[:, cc, 1 + j:513 + j], start=(j == 0), stop=(j == 30)),
                            reads=([("diag", dbuf)] + (gkeys if j == 0 else [])) if j == j0 else [],
                            writes=[("ps", bank)] if j == 0 else [], signal=(j == j1 - 1))
                    B._commit(B.last["pe"], [("diag", dbuf)], [])
                B._commit(B.last["pe"], gkeys, [("ps", bank)])
                B.op("act", lambda e, bank=bank: e.activation(out=convb[:, cc, :], in_=psb(bank), func=AF.Identity,
                                                               bias=cv[:, C_DWB + cc:C_DWB + cc + 1], scale=1.0),
                     reads=[("ps", bank), "cv"], writes=[("conv", cc)])
                B.op("act", lambda e: e.activation(out=sqb[:, 0, :], in_=convb[:, cc, :], func=AF.Square),
                     reads=[("conv", cc)], writes=["sq"])
                B.op("act", lambda e: e.copy(out=hlb[:, 0, :], in_=convb[:, cc, :]), reads=[("conv", cc)], writes=[("hl", 0)])
                B.op("dve", lambda e: e.tensor_tensor(out=hlb[:, 1, :], in0=convb[:, cc, :], in1=hlb[:, 0, :], op=ALU.subtract),
                     reads=[("conv", cc), ("hl", 0)], writes=[("hl", 1)])
                B.op("act", lambda e: e.copy(out=hlb[:, 2, :], in_=sqb[:, 0, :]), reads=["sq"], writes=[("hl", 2)])
                B.op("dve", lambda e: e.tensor_tensor(out=hlb[:, 3, :], in0=sqb[:, 0, :], in1=hlb[:, 2, :], op=ALU.subtract),
                     reads=["sq", ("hl", 2)], writes=[("hl", 3)])

            def stat_unit(cc):
                for i in range(4):
                    bank = 6 + i // 2
                    B.op("pe", lambda e, i=i, bank=bank: e.matmul(psb(bank), ones[:, :], hlb[:, i, :],
                                                                  start=(cc == 0 and i % 2 == 0), stop=(cc == 7 and i % 2 == 1)),
                         reads=[("hl", i), "ones"], writes=[("ps", bank)] if (cc == 0 and i % 2 == 0) else [])
                if cc == 7:
                    B._commit(B.last["pe"], [], [("ps", 6), ("ps", 7)])

            glu_unit(0)
            for cc in range(1, 8):
                glu_unit(cc)
                if MC <= 2:
                    continue
                if cc >= 2:
                    stat_unit(cc - 2)
                conv_unit(cc - 1)
            if MC <= 2:
                return
            stat_unit(6)
            conv_unit(7)
            if MC <= 3:
                stat_unit(7)
                return

            def pool_chunk(cc):
                su = load_tile(T_WIN + cc)
                ub = cc % 2
                for (t0, tn) in ET:
                    b = nb()
                    win_unit(su, t0, tn, b)
                    B.op("act", lambda e, b=b, t0=t0, tn=tn: e.copy(out=ubuf[:, ub, t0:t0 + tn], in_=psb(b, tn)),
                         reads=[("ps", b)], writes=[("ub", ub, t0)])
                if cc == 0:
                    stat_unit(7)
                gi = cc // 2
                w = WINDOWS[gi]
                steps = int(np.log2(w))
                cur = ubuf[:, ub, :]
                curk = [("ub", ub, 0), ("ub", ub, 272)]
                L = HE
                for k in range(steps):
                    sh = 1 << k
                    dst = ptb[:, k % 2, :]
                    B.op("dve", lambda e, cur=cur, dst=dst, L=L, sh=sh: e.tensor_tensor(
                        out=dst[:, 0:L - sh], in0=cur[:, 0:L - sh], in1=cur[:, sh:L], op=ALU.add),
                        reads=curk, writes=[("pt", k % 2)])
                    cur, curk, L = dst, [("pt", k % 2)], L - sh
                o0 = 16 - w // 2
                B.op("dve", lambda e, cur=cur, o0=o0, w=w: e.scalar_tensor_tensor(
                    out=pooled[:, cc, :], in0=cur[:, o0:o0 + 512], scalar=1.0 / w, in1=ubuf[:, ub, 16:528],
                    op0=ALU.mult, op1=ALU.subtract),
                    reads=curk + [("ub", ub, 0), ("ub", ub, 272)], writes=[("pooled", cc)])
                if half == 0:
                    a0, r0 = 0, gi * 16
                else:
                    a0, r0 = 504, gi * 16 + 8
                B.op("dve", lambda e, cur=cur, o0=o0, a0=a0, r0=r0: e.tensor_tensor(
                    out=tmp8[:, 0:8], in0=cur[:, o0 + a0:o0 + a0 + 8], in1=rc[:, r0:r0 + 8], op=ALU.mult),
                    reads=curk + ["rc"], writes=["tmp8"])
                B.op("dve", lambda e, a0=a0: e.tensor_tensor(
                    out=pooled[:, cc, a0:a0 + 8], in0=tmp8[:, 0:8], in1=ubuf[:, ub, 16 + a0:16 + a0 + 8], op=ALU.subtract),
                    reads=["tmp8", ("ub", ub, 0), ("ub", ub, 272)], writes=[("pooled", cc)])
            def grp_unit(gi):
                if True:
                    sg_slot = load_tile(T_PGRP)
                    for oc in range(2):
                        b = nb()
                        for kc in range(2):
                            B.op("pe", lambda e, kc=kc, oc=oc, b=b: e.matmul(
                                psb(b), ring[:, sg_slot, gi * 512 + kc * 256 + oc * 128: gi * 512 + kc * 256 + oc * 128 + 128],
                                pooled[:, 2 * gi + kc, :], start=(kc == 0), stop=(kc == 1)),
                                reads=[("G", sg_slot), ("pooled", 2 * gi), ("pooled", 2 * gi + 1)] if kc == 0 else [],
                                writes=[("ps", b)] if kc == 0 else [], signal=(kc == 1))
                        B._commit(B.last["pe"], [("G", sg_slot), ("pooled", 2 * gi), ("pooled", 2 * gi + 1)], [("ps", b)])
                        mc = 2 * gi + oc
                        B.op("dve", lambda e, b=b, mc=mc: e.tensor_scalar(
                            out=mixedT[:, mc, :], in0=psb(b), scalar1=cv[:, C_PS + mc:C_PS + mc + 1], scalar2=None, op0=ALU.mult),
                            reads=[("ps", b), "cv"], writes=[("mixed", mc)])

            for cc_ in range(8):
                pool_chunk(cc_)
                if cc_ >= 3 and cc_ % 2 == 1:
                    grp_unit((cc_ - 3) // 2)
            grp_unit(3)

            if MC <= 4:
                return
            B.op("dve", lambda e: e.tensor_scalar(out=lnm[:, 0, :], in0=psb(6), scalar1=1.0 / 1024, scalar2=None, op0=ALU.mult),
                 reads=[("ps", 6)], writes=["mean"])
            B.op("dve", lambda e: e.tensor_tensor(out=lnm[:, 1, :], in0=lnm[:, 0, :], in1=lnm[:, 0, :], op=ALU.mult),
                 reads=["mean"], writes=["msq"])
            B.op("dve", lambda e: e.scalar_tensor_tensor(out=lnm[:, 1, :], in0=psb(7), scalar=1.0 / 1024, in1=lnm[:, 1, :],
                                                         op0=ALU.mult, op1=ALU.subtract),
                 reads=[("ps", 7), "msq"], writes=["msq"])
            B.op("dve", lambda e: e.tensor_scalar(out=lnm[:, 1, :], in0=lnm[:, 1, :], scalar1=EPS, scalar2=None, op0=ALU.add),
                 reads=["msq"], writes=["msq"])
            B.op("act", lambda e: e.activation(out=lnm[:, 1, :], in_=lnm[:, 1, :], func=AF.Sqrt),
                 reads=["msq"], writes=["msq"])
            B.op("dve", lambda e: e.reciprocal(out=lnm[:, 2, :], in_=lnm[:, 1, :]),
                 reads=["msq"], writes=["lrstd"])
            for cc in range(8):
                li = cc % 2
                B.op("dve", lambda e, li=li, cc=cc: e.tensor_tensor(out=ltb[:, li, :], in0=convb[:, cc, :], in1=lnm[:, 0, :], op=ALU.subtract),
                     reads=[("conv", cc), "mean"], writes=[("lt", li)])
                B.op("dve", lambda e, li=li: e.tensor_tensor(out=ltb[:, li, :], in0=ltb[:, li, :], in1=lnm[:, 2, :], op=ALU.mult),
                     reads=[("lt", li), "lrstd"], writes=[("lt", li)])
                B.op("act", lambda e, li=li, cc=cc: e.activation(out=ybf[:, cc, :], in_=ltb[:, li, :], func=AF.Silu,
                                                                 bias=cv[:, C_LNB + cc:C_LNB + cc + 1],
                                                                 scale=cv[:, C_LNG + cc:C_LNG + cc + 1]),
                     reads=[("lt", li), "cv"],
                     writes=[("y", cc)] + ([("glu", c_, t_) for c_ in range(8) for t_ in (0, 272)] if cc == 0 else []))

            if MC <= 5:
                return
            OWN0 = 16
            ykeys = [("y", c_) for c_ in range(8)]
            mkeys = [("mixed", c_) for c_ in range(8)]
            ckeys = [("conv", c_) for c_ in range(8)]
            gi_ = [0]

            def sig_unit(tile, col_bias):
                sl_ = load_tile(tile)
                bk = nb()
                g = gi_[0] % 2
                gi_[0] += 1
                win_unit(sl_, OWN0, 512, bk)
                B.op("act", lambda e: e.activation(out=sgm[:, g, :], in_=psb(bk), func=AF.Sigmoid,
                                                   bias=cv[:, col_bias:col_bias + 1], scale=1.0),
                     reads=[("ps", bk), "cv"], writes=[("sgm", g)])
                return g

            def proj_units(tile, src, skeys, dp):
                sl_ = load_tile(tile)
                bks = [nb(), nb()]
                for o in range(2):
                    for kc in range(8):
                        B.op("pe", lambda e, kc=kc, o=o: e.matmul(
                            psb(bks[o]), ring[:, sl_, o * 1024 + kc * 128:o * 1024 + kc * 128 + 128],
                            src[:, kc, :], start=(kc == 0), stop=(kc == 7)),
                            reads=([("G", sl_)] + skeys) if kc == 0 else [], writes=[("ps", bks[o])] if kc == 0 else [],
                            signal=(kc == 7))
                    B._commit(B.last["pe"], [("G", sl_)] + skeys, [("ps", bks[o])])
                return bks

            def gate_a(dc, bk, g):
                B.op("dve", lambda e: e.tensor_tensor(out=gt[:, g, :], in0=psb(bk), in1=sgm[:, g, :], op=ALU.mult),
                     reads=[("ps", bk), ("sgm", g)], writes=[("gt", g)])

            def gate_b(dc, bk, g, ga_):
                B.op("dve", lambda e: e.scalar_tensor_tensor(
                    out=gt[:, 2 + g, :], in0=psb(bk), scalar=cv[:, C_BP + dc:C_BP + dc + 1], in1=sgm[:, g, :],
                    op0=ALU.add, op1=ALU.mult),
                    reads=[("ps", bk), ("sgm", g), "cv"], writes=[("gt", 2 + g)])
                B.op("dve", lambda e: e.tensor_tensor(out=mbf[:, dc, :], in0=gt[:, ga_, :], in1=gt[:, 2 + g, :], op=ALU.add),
                     reads=[("gt", ga_), ("gt", 2 + g)], writes=[("m", dc)] + (ckeys if dc == 0 else []))

            for dp in range(8):
                dc0, dc1 = 2 * dp, 2 * dp + 1
                g = sig_unit(T_WIN + 24 + dc0, C_BG + dc0)
                abk = proj_units(T_PPROJ + dp, mixedT, mkeys, dp)
                gate_a(dc0, abk[0], g)
                ga0 = g
                g = sig_unit(T_WIN + 40 + dc0, C_BG + 16 + dc0)
                bbk = proj_units(T_CPROJ + dp, ybf, ykeys, dp)
                gate_b(dc0, bbk[0], g, ga0)
                g = sig_unit(T_WIN + 24 + dc1, C_BG + dc1)
                gate_a(dc1, abk[1], g)
                ga1 = g
                g = sig_unit(T_WIN + 40 + dc1, C_BG + 16 + dc1)
                gate_b(dc1, bbk[1], g, ga1)

            if MC <= 6:
                return
            mk = [("m", d_) for d_ in range(16)]
            for dt in range(4):
                bks = [0, 1, 2, 3] if dt % 2 == 0 else [4, 5, 6, 7]
                bkeys = [("ps", b_) for b_ in bks]
                for kq in range(4):
                    sl_ = load_tile(T_WOUT + dt * 4 + kq)
                    n = 0
                    for c in range(4):
                        for k4 in range(4):
                            kc = kq * 4 + k4
                            B.op("pe", lambda e, kc=kc, k4=k4, c=c, sl_=sl_, bks=bks: e.matmul(
                                psb(bks[c]), mbf[:, kc, c * 128:(c + 1) * 128],
                                ring[:, sl_, k4 * 512:(k4 + 1) * 512], start=(kc == 0), stop=(kc == 15)),
                                reads=([("G", sl_)] + mk) if n == 0 else [],
                                writes=bkeys if (n == 0 and kq == 0) else [], signal=(n == 15))
                            n += 1
                    B._commit(B.last["pe"], [("G", sl_)] + mk, bkeys if kq == 3 else [])
                for c in range(4):
                    gc = gc0 + c
                    B.op("dve", lambda e, c=c, gc=gc, dt=dt, bks=bks: e.tensor_tensor(
                        out=xres[:, gc, dt * 512:(dt + 1) * 512], in0=psb(bks[c]), in1=xres[:, gc, dt * 512:(dt + 1) * 512], op=ALU.add),
                        reads=[("ps", bks[c])], writes=[("x", gc)])

        if stage >= 2:
            mixer(0)
            bar = phase_barrier()
            mixer(1)
            bar = phase_barrier()

        if stage >= 3:
            ffn_dextra.extend(bar)
            load_gain(2)
            prep_seq([(c, [(lambda k0, c=c: hT[:, k0:k0 + 4, c * 128:(c + 1) * 128], 0, 128, [("hT", c)])], True)
                      for c in range(8)])
            ffn(1, NOWN, [(0, 512), (512, 512)], 8)
            bar = phase_barrier()

        load_gain(3)
        for c in range(8):
            rstd_chunk(c)
            B.op("dve", lambda e, c=c: e.scalar_tensor_tensor(out=xres[:, c, :], in0=xres[:, c, :], scalar=rsb[:, c:c + 1],
                                                              in1=gbc[:, :], op0=ALU.mult, op1=ALU.mult),
                 reads=[("x", c), ("rs", c), "gbc"], writes=[("x", c)])
            B.dma("sp", lambda e, c=c: e.dma_start(out=out[c * 128:(c + 1) * 128, :], in_=xres[:, c, :]),
                  s_out, 16 * (c + 1), reads=[("x", c)])
        B.wait_only("sp", [(s_out, 16 * 8)])

        with nc.Block() as block:
            @block.sync
            def _(e):
                B.replay("sp", e)

            @block.gpsimd
            def _(e):
                B.replay("pool", e)

            @block.tensor
            def _(e):
                B.replay("pe", e)

            @block.scalar
            def _(e):
                B.replay("act", e)

            @block.vector
            def _(e):
                B.replay("dve", e)
    return nc


def _kmajor(W):
    K, O = W.shape
    return W.reshape(K // 128, 128, O // 128, 128).transpose(2, 1, 0, 3).reshape(O // 128, 128, K)


def build_weight_stream(inp):
    wst = np.empty((NT, 128, 2048), np.float32)
    for f, pre in enumerate(("ffn1", "ffn2")):
        v = wst[f * 132:(f + 1) * 132].reshape(NJ, 3, 128, 2048)
        v[:, 0] = _kmajor(np.asarray(inp[pre + "_w_gate"][0]))
        v[:, 1] = _kmajor(np.asarray(inp[pre + "_w_up"][0]))
        v[:, 2] = np.asarray(inp[pre + "_w_down"][0]).reshape(NJ, 128, 2048)
    wst[T_WIN:T_WIN + 56] = _kmajor(np.asarray(inp["w_in"][0]))
    wg = np.asarray(inp["pool_w_group"][0])
    wst[T_PGRP] = wg.reshape(4, 2, 128, 256).transpose(2, 0, 1, 3).reshape(128, 2048)
    pp = _kmajor(np.asarray(inp["pool_w_proj"][0]))
    wst[T_PPROJ:T_PPROJ + 8] = pp.reshape(8, 2, 128, 1024).transpose(0, 2, 1, 3).reshape(8, 128, 2048)
    cp = _kmajor(np.asarray(inp["conv_w_proj"][0]))
    wst[T_CPROJ:T_CPROJ + 8] = cp.reshape(8, 2, 128, 1024).transpose(0, 2, 1, 3).reshape(8, 128, 2048)
    wo = np.asarray(inp["w_out"][0])
    wst[T_WOUT:T_WOUT + 16] = wo.reshape(4, 4, 128, 4, 512).transpose(3, 0, 2, 1, 4).reshape(16, 128, 2048)
    return wst


def _col(v):
    v = np.asarray(v, np.float32).reshape(-1)
    return v.reshape(-1, 128).T


def build_inputs(inp):
    x = np.asarray(inp["x"], np.float32)
    Bsz, T, _ = x.shape
    wst = build_weight_stream(inp)
    gains = np.stack([np.asarray(inp["ffn1_norm"][0]), np.asarray(inp["mix_norm"][0]),
                      np.asarray(inp["ffn2_norm"][0]), np.asarray(inp["final_norm"])]).astype(np.float32)
    cvec = np.zeros((128, NCV), np.float32)
    cvec[:, C_BG:C_BG + 32] = _col(inp["b_gate"][0])
    cvec[:, C_PS:C_PS + 8] = _col(inp["pool_scale"][0])
    cvec[:, C_DWB:C_DWB + 8] = _col(inp["conv_dw_b"][0])
    cvec[:, C_LNG:C_LNG + 8] = _col(inp["conv_ln_g"][0])
    cvec[:, C_LNB:C_LNB + 8] = _col(inp["conv_ln_b"][0])
    cvec[:, C_BP:C_BP + 16] = _col(inp["conv_b_proj"][0])
    dww = np.asarray(inp["conv_dw_w"][0], np.float32)
    cvec[:, C_DWW:] = dww.reshape(31, 8, 128).transpose(2, 1, 0).reshape(128, 248)
    ident = np.eye(128, dtype=np.float32)
    maps = []
    for core in range(8):
        b, q = core // 4, core % 4
        t0 = q * NOWN
        xin = np.zeros((NEXT, D), np.float32)
        xin[0:NOWN] = x[b, t0:t0 + NOWN]
        if t0 - HALO >= 0:
            xin[NOWN:NOWN + HALO] = x[b, t0 - HALO:t0]
        if t0 + NOWN + HALO <= T:
            xin[NOWN + HALO:] = x[b, t0 + NOWN:t0 + NOWN + HALO]
        rcv = np.zeros((4, 16), np.float32)
        for wi, w in enumerate(WINDOWS):
            for i in range(16):
                g = t0 + (i if i < 8 else NOWN - 16 + i)
                lo = max(g - w // 2, 0)
                hi = min(g + (w - w // 2), T)
                rcv[wi, i] = 1.0 / float(hi - lo)
        rcin = np.broadcast_to(rcv.reshape(1, 64), (128, 64)).copy()
        maps.append({"xin": xin, "wst": wst, "gains": gains, "cvec": cvec, "rcin": rcin, "idin": ident})
    return maps


_NC_CACHE = {}


def kernel(**inputs):
    maps = build_inputs(inputs)
    if "nc" not in _NC_CACHE:
        _NC_CACHE["nc"] = build_program()
    nc = _NC_CACHE["nc"]
    res = run_bass_kernel_spmd(nc, maps, core_ids=list(range(8)))
    x = np.asarray(inputs["x"])
    outp = np.empty(x.shape, np.float32)
    for core in range(8):
        b, q = core // 4, core % 4
        outp[b, q * NOWN:(q + 1) * NOWN] = res.results[core]["out"]
    return outp
```

```python
import contextlib
import os
import numpy as np
import concourse.bass as bass
import concourse.mybir as mybir
from concourse.bass_utils import run_bass_kernel_spmd

F32 = mybir.dt.float32
BF16 = mybir.dt.bfloat16
AF = mybir.ActivationFunctionType
ALU = mybir.AluOpType

D = 2048
DFF = 5632
NJ = DFF // 128
SL = 4
NSLAB = NJ // SL
EPS = 1e-6
HALO = 16
NOWN = 1024
NEXT = NOWN + 2 * HALO
HE = 512 + 2 * HALO
WINDOWS = (2, 4, 8, 16)

def T_G(f, j): return f * 132 + 3 * j
def T_U(f, j): return f * 132 + 3 * j + 1
def T_D(f, j): return f * 132 + 3 * j + 2
T_WIN = 264
T_PGRP = 320
T_PPROJ = 321
T_CPROJ = 329
T_WOUT = 337
NT = 353

C_BG, C_PS, C_DWB, C_LNG, C_LNB, C_BP, C_DWW = 0, 32, 40, 48, 56, 64, 80
NCV = 80 + 8 * 31

RG = 6
RD = 8


class Builder:
    def __init__(self, nc, sems):
        self.nc = nc
        self.q = {e: [] for e in ("pe", "act", "dve", "pool", "sp")}
        self.prog = {e: [sems[e], 0] for e in ("pe", "act", "dve")}
        self.trk = {}
        self.last = {e: None for e in ("pe", "act", "dve")}

    def _deps(self, reads, writes):
        w = []
        for k in reads:
            t = self.trk.get(k)
            if t and t["w"] is not None:
                w.append(t["w"])
        for k in writes:
            t = self.trk.get(k)
            if t:
                if t["w"] is not None:
                    w.append(t["w"])
                w.extend(t["r"])
        return w

    def _commit(self, tok, reads, writes):
        for k in reads:
            self.trk.setdefault(k, {"w": None, "r": []})["r"].append(tok)
        for k in writes:
            self.trk[k] = {"w": tok, "r": []}

    def op(self, eng, fn, reads=(), writes=(), extra=(), signal=True):
        waits = self._deps(reads, writes) + [t for t in extra if t is not None]
        tok = None
        if signal:
            p = self.prog[eng]
            p[1] += 1
            tok = (p[0], p[1])
            self.last[eng] = tok
        self.q[eng].append((waits, fn, tok))
        if signal:
            self._commit(tok, reads, writes)
        return tok

    def dma(self, eng, fn, sem, count, reads=(), writes=(), extra=()):
        waits = self._deps(reads, writes) + [t for t in extra if t is not None]
        tok = (sem, count)
        self.q[eng].append((waits, fn, ("dma", sem)))
        self._commit(tok, reads, writes)
        return tok

    def wait_only(self, eng, toks):
        self.q[eng].append(([t for t in toks if t is not None], None, None))

    def replay(self, eng, e):
        seen = {}
        for waits, fn, tok in self.q[eng]:
            for (s, v) in waits:
                key = id(s)
                if seen.get(key, 0) >= v:
                    continue
                seen[key] = v
                e.wait_ge(s, v)
            if fn is None:
                continue
            ins = fn(e)
            if tok is not None:
                if tok[0] == "dma":
                    ins.then_inc(tok[1], 16)
                else:
                    ins.then_inc(tok[0], 1)


def build_program(stage=99):
    nc = bass.Bass("TRN2", target_bir_lowering=False)
    xin = nc.dram_tensor("xin", [NEXT, D], F32, kind="ExternalInput").ap()
    tbase = 264 if os.environ.get('KSKIPF') else 0
    nt_used = int(os.environ['KNT']) if os.environ.get('KNT') else (NT - tbase) if stage >= 2 else (min(132, 12 * int(os.environ.get('KSLABS', 11))) if stage >= 1 else 1)
    wst = nc.dram_tensor("wst", [nt_used, 128, 2048], F32, kind="ExternalInput").ap()
    gains = nc.dram_tensor("gains", [4, D], F32, kind="ExternalInput").ap()
    cvec = nc.dram_tensor("cvec", [128, NCV], F32, kind="ExternalInput").ap()
    rcin = nc.dram_tensor("rcin", [128, 64], F32, kind="ExternalInput").ap()
    idin = nc.dram_tensor("idin", [128, 128], F32, kind="ExternalInput").ap()
    out = nc.dram_tensor("out", [NOWN, D], F32, kind="ExternalOutput").ap()

    with contextlib.ExitStack() as es:
        def sb(name, shape, dt):
            return es.enter_context(nc.sbuf_tensor(name, shape, dt))

        def sem(name):
            return es.enter_context(nc.semaphore(name))

        xres = sb("xres", [128, 8, D], F32)
        ring = sb("ring", [128, RG, 2048], BF16)
        hTa = sb("hTa", [128, 16 * NEXT], BF16)
        gbc = sb("gbc", [128, D], F32)
        xhat = sb("xhat", [128, D], BF16)
        cv = sb("cv", [128, NCV], F32)
        rc = sb("rc", [128, 64], F32)
        identf = sb("identf", [128, 128], F32)
        identb = sb("identb", [128, 128], BF16)
        ones = sb("ones", [128, 128], BF16)
        ssb = sb("ssb", [128, 16], F32)
        rsb = sb("rsb", [128, 16], F32)
        stash = sb("stash", [128, 16, 32], BF16)
        diag = sb("diag", [128, 16, 128], BF16)
        arena = sb("arena", [128, 16384], F32)
        ps = es.enter_context(nc.psum_tensor("ps", [128, 8, 512], F32))

        sems = {"pe": sem("s_pe"), "act": sem("s_act"), "dve": sem("s_dve")}
        s_ring = [sem(f"s_rg{i}") for i in range(RG)]
        s_dring = [sem(f"s_rd{i}") for i in range(RD)]
        s_x = [sem(f"s_x{i}") for i in range(9)]
        s_c = sem("s_const")
        s_gn = sem("s_gain")
        s_out = sem("s_out")

        B = Builder(nc, sems)

        def aview(off_bytes, shape, dt):
            n = int(np.prod(shape[1:]))
            if dt == F32:
                a = arena[:, off_bytes // 4: off_bytes // 4 + n]
            else:
                a = arena[:, off_bytes // 4: off_bytes // 4 + n // 2].bitcast(BF16)
            if len(shape) == 3:
                a = a.rearrange("p (a b) -> p a b", a=shape[1])
            elif len(shape) == 4:
                a = a.rearrange("p (a b c) -> p a b c", a=shape[1], b=shape[2])
            return a

        xh = aview(0, [128, D], F32)
        dring = aview(8192, [128, RD, 2048], BF16)
        actb = aview(40960, [128, 2, SL, NEXT], BF16)
        sgb = aview(57856, [128, 2, 512], F32)
        glu = aview(0, [128, 8, HE], BF16)
        ybf = aview(0, [128, 8, 512], BF16)
        convb = aview(8704, [128, 8, 512], F32)
        mbf = aview(8704, [128, 16, 512], BF16)
        ubuf = aview(25088, [128, 2, HE], F32)
        ptb = aview(29440, [128, 2, HE], F32)
        sgm = aview(33792, [128, 2, 512], F32)
        sqb = aview(37888, [128, 1, 512], F32)
        hlb = aview(60480, [128, 4, 512], BF16)
        lnm = aview(41984, [128, 3, 512], F32)
        gt = aview(48128, [128, 4, 512], F32)
        ltb = aview(56320, [128, 2, 512], F32)
        tmp8 = aview(60416, [128, 16], F32)
        xhat2 = aview(45056, [128, D], BF16)
        junkA = aview(40960, [128, D], BF16)

        hT = hTa[:, :].rearrange("p (k t) -> p k t", k=16)
        hTm = hTa[:, 0:16 * HE].rearrange("p (k t) -> p k t", k=16)
        pooled = hTa[:, 16 * HE:16 * HE + 8 * 512].rearrange("p (k t) -> p k t", k=8)
        mixedT = hTa[:, 16 * HE + 8 * 512:16 * HE + 16 * 512].rearrange("p (k t) -> p k t", k=8)

        def psb(b, n=512):
            return ps[:, b, 0:n]

        def psT(b):
            return ps[:, b, :].bitcast(BF16).rearrange("p (k t) -> p k t", k=8)

        gstate = {"n": 0, "cnt": [0] * RG}
        dstate = {"n": 0, "cnt": [0] * RD}

        def load_tile(tid, dring_=False, extra=()):
            st, sems_, buf, R, nm = ((dstate, s_dring, dring, RD, "D") if dring_
                                     else (gstate, s_ring, ring, RG, "G"))
            slot = st["n"] % R
            st["n"] += 1
            st["cnt"][slot] += 1
            B.dma("pool", lambda e, s=slot, t=tid, b=buf: e.dma_start(out=b[:, s, :], in_=wst[t - tbase]),
                  sems_[slot], 16 * st["cnt"][slot], writes=[(nm, slot)], extra=extra)
            return slot

        B.dma("sp", lambda e: e.dma_start(out=cv[:, :], in_=cvec), s_c, 48, writes=["cv"])
        B.dma("sp", lambda e: e.dma_start(out=rc[:, :], in_=rcin), s_c, 48, writes=["rc"])
        B.dma("sp", lambda e: e.dma_start(out=identf[:, :], in_=idin), s_c, 48, writes=["identf"])
        B.op("dve", lambda e: e.tensor_copy(out=identb[:, :], in_=identf[:, :]), reads=["identf"], writes=["identb"])
        B.op("dve", lambda e: e.memset(ones[:, :], 1.0), writes=["ones"])
        gain_n = [0]

        def load_gain(i):
            gain_n[0] += 1
            B.dma("sp", lambda e, i=i: e.dma_start(out=gbc[:, :], in_=gains[i:i + 1, :].broadcast_to([128, D])),
                  s_gn, 16 * gain_n[0], writes=["gbc"])

        load_gain(0)
        for c in range(8):
            B.dma("sp", lambda e, c=c: e.dma_start(out=xres[:, c, :], in_=xin[c * 128:(c + 1) * 128, :]),
                  s_x[c], 16, writes=[("x", c)])
        B.dma("sp", lambda e: e.dma_start(out=xh[0:32, :], in_=xin[1024:1056, :]), s_x[8], 16, writes=[("x", 8)])

        def xchunk(c, P=128):
            return xh[0:32, :] if c == 8 else xres[:, c, :]

        def rstd_chunk(c):
            P = 32 if c == 8 else 128
            x = xchunk(c)
            B.op("act", lambda e: e.activation(out=junkA[0:P, :], in_=x, func=AF.Square,
                                               accum_out=ssb[0:P, c:c + 1]),
                 reads=[("x", c)], writes=["junkA", ("ss", c)])
            B.op("dve", lambda e: e.tensor_scalar(out=rsb[0:P, c:c + 1], in0=ssb[0:P, c:c + 1],
                                                  scalar1=1.0 / D, scalar2=EPS, op0=ALU.mult, op1=ALU.add),
                 reads=[("ss", c)], writes=[("rs", c)])
            B.op("act", lambda e: e.activation(out=ssb[0:P, c:c + 1], in_=rsb[0:P, c:c + 1], func=AF.Sqrt),
                 reads=[("rs", c)], writes=[("ss", c)])
            B.op("dve", lambda e: e.reciprocal(out=rsb[0:P, c:c + 1], in_=ssb[0:P, c:c + 1]),
                 reads=[("ss", c)], writes=[("rs", c)])

        tbank = [0]
        pchunk = [0]

        def prep_seq(items):
            LA = 2
            for (c, dests, cr) in items[:LA]:
                if cr:
                    rstd_chunk(c)
            pend = []
            for i, (c, dests, cr) in enumerate(items):
                if i + LA < len(items) and items[i + LA][2]:
                    rstd_chunk(items[i + LA][0])
                ev = prep_xt(c, dests)
                for f in pend:
                    f()
                pend = ev
            for f in pend:
                f()

        def prep_xt(c, dests):
            P = 32 if c == 8 else 128
            evs = []
            x = xchunk(c)
            xi = pchunk[0] % 2
            pchunk[0] += 1
            xhat_ = xhat if xi == 0 else xhat2
            xkey = ("xhat", xi)
            B.op("dve", lambda e: e.scalar_tensor_tensor(out=xhat_[0:P, :], in0=x, scalar=rsb[0:P, c:c + 1],
                                                         in1=gbc[0:P, :], op0=ALU.mult, op1=ALU.mult),
                 reads=[("x", c), ("rs", c), "gbc"], writes=[xkey])
            for q in range(4):
                b = tbank[0] % 8
                tbank[0] += 1
                pt = ps[:, b, :].rearrange("p (k t) -> p k t", k=4)
                for k in range(4):
                    kk = q * 4 + k
                    B.op("pe", lambda e, k=k, kk=kk, pt=pt: e.matmul(pt[:, k, 0:P], xhat_[0:P, kk * 128:(kk + 1) * 128],
                                                                     identb[0:P, 0:P], start=True, stop=True),
                         reads=[xkey, "identb"] if k == 0 else [], writes=[("ps", b)] if k == 0 else [],
                         signal=(k == 3))
                B._commit(B.last["pe"], [xkey], [("ps", b)])
                for di, (dst, lo, hi, keys) in enumerate(dests):
                    if q % 2 == 0:
                        evs.append(lambda dst=dst, lo=lo, hi=hi, pt=pt, q=q, b=b, keys=keys: B.op(
                            "act", lambda e: e.copy(out=dst(q * 4), in_=pt[:, :, lo:hi]),
                            reads=[("ps", b)], writes=[(k_, q) for k_ in keys]))
                    else:
                        evs.append(lambda dst=dst, lo=lo, hi=hi, pt=pt, q=q, b=b, keys=keys: B.op(
                            "dve", lambda e: e.tensor_copy(out=dst(q * 4), in_=pt[:, :, lo:hi]),
                            reads=[("ps", b)], writes=[(k_, q) for k_ in keys]))
            return evs

        def ffn(f, ntok, ttiles, nchunks):
            gu_banks = [0, 1, 2, 3]
            gub = [0]
            sgi = [0]

            def hkeys(s, e):
                ks = []
                for blk in range(9):
                    a, b_ = (blk * 128, blk * 128 + 128) if blk < 8 else (1024, 1056)
                    if a < e and b_ > s and a < ntok:
                        ks += [(("hT", blk), q_) for q_ in range(4)]
                return ks

            def gu_unit(s, jl, j, slot_g, slot_u, ti):
                t0, tn = ttiles[ti]
                bg = gu_banks[gub[0] % 4]
                bu = gu_banks[(gub[0] + 1) % 4]
                gub[0] += 2
                for which, slot, bank in (("g", slot_g, bg), ("u", slot_u, bu)):
                    for kc in range(16):
                        first, last = kc == 0, kc == 15
                        B.op("pe", lambda e, kc=kc, slot=slot, bank=bank, first=first, last=last:
                             e.matmul(psb(bank, tn), ring[:, slot, kc * 128:(kc + 1) * 128],
                                      hT[:, kc, t0:t0 + tn], start=first, stop=last),
                             reads=([("G", slot)] + hkeys(t0, t0 + tn)) if first else [],
                             writes=[("ps", bank)] if first else [], signal=last)
                    B._commit(B.last["pe"], [("G", slot)] + hkeys(t0, t0 + tn), [("ps", bank)])
                si = sgi[0] % 2
                sgi[0] += 1
                B.op("act", lambda e: e.activation(out=sgb[:, si, 0:tn], in_=psb(bg, tn), func=AF.Silu),
                     reads=[("ps", bg)], writes=[("sg", si)])
                B.op("dve", lambda e: e.tensor_tensor(out=actb[:, s % 2, jl, t0:t0 + tn], in0=psb(bu, tn),
                                                      in1=sgb[:, si, 0:tn], op=ALU.mult),
                     reads=[("ps", bu), ("sg", si)], writes=[("act", s % 2, jl, ti)])

            dcnt = [0]

            def down_unit(s, cd, dslots):
                c, dh = cd
                P = 32 if c == 8 else 128
                c0 = 1024 if c == 8 else c * 128
                akeys = [("act", s % 2, jl, ti) for jl in range(SL) for ti, (t0, tn) in enumerate(ttiles)
                         if t0 < c0 + P and t0 + tn > c0]
                dkeys = [("D", sl_) for sl_ in dslots]
                bks = (4, 5) if dcnt[0] % 2 == 0 else (6, 7)
                dcnt[0] += 1
                n = 0
                for jl in range(SL):
                    for d2 in range(2):
                        first, last = jl == 0, (jl == SL - 1 and d2 == 1)
                        dcol = (dh * 2 + d2) * 512
                        B.op("pe", lambda e, jl=jl, d2=d2, first=first, dcol=dcol:
                             e.matmul(ps[0:P, bks[d2], :], actb[:, s % 2, jl, c0:c0 + P],
                                      dring[:, dslots[jl], dcol:dcol + 512],
                                      start=first, stop=(jl == SL - 1)),
                             reads=(akeys + dkeys) if n == 0 else [],
                             writes=[("ps", bks[0]), ("ps", bks[1])] if n == 0 else [], signal=last)
                        n += 1
                B._commit(B.last["pe"], akeys + dkeys, [("ps", bks[0]), ("ps", bks[1])])
                xc = xchunk(c)
                for d2 in range(2):
                    dcol = (dh * 2 + d2) * 512
                    B.op("dve", lambda e, d2=d2, dcol=dcol: e.scalar_tensor_tensor(
                        out=xc[0:P, dcol:dcol + 512], in0=ps[0:P, bks[d2], :], scalar=0.5,
                        in1=xc[0:P, dcol:dcol + 512], op0=ALU.mult, op1=ALU.add),
                        reads=[("ps", bks[d2])], writes=[("x", c)])

            pending_down = []
            nslab = int(os.environ.get('KSLABS', NSLAB))
            for s in range(nslab + 1):
                gus = []
                if s < nslab:
                    for jl in range(SL):
                        for ti in range(len(ttiles)):
                            gus.append((jl, ti))
                downs = list(pending_down)
                pending_down = []
                dslots_prev = getattr(ffn, "_dsl", None)
                ngu, nd = len(gus), len(downs)
                slots = {}
                di = 0
                for gi, (jl, ti) in enumerate(gus):
                    j = s * SL + jl
                    if ti == 0:
                        slots[jl] = (load_tile(T_G(f, j)), load_tile(T_U(f, j)))
                    gu_unit(s, jl, j, slots[jl][0], slots[jl][1], ti)
                    want = ((gi + 1) * nd) // ngu if ngu else nd
                    while di < want:
                        down_unit(s - 1, downs[di], dslots_prev)
                        di += 1
                while di < nd:
                    down_unit(s - 1, downs[di], dslots_prev)
                    di += 1
                if s < nslab:
                    ffn._dsl = [load_tile(T_D(f, s * SL + jl), dring_=True, extra=ffn_dextra) for jl in range(SL)]
                    pending_down = [(c_, dh_) for c_ in range(nchunks) for dh_ in range(2)]

        ffn_dextra = []

        def phase_barrier():
            toks = [B.last["pe"], B.last["act"], B.last["dve"]]
            for eng in ("pe", "act", "dve"):
                B.wait_only(eng, toks)
            return toks

        if stage >= 0.5:
            items = []
            for c in range(9):
                n = 32 if c == 8 else 128
                c0 = 1024 if c == 8 else c * 128
                items.append((c, [(lambda k0, c0=c0, n=n: hT[:, k0:k0 + 4, c0:c0 + n], 0, n, [("hT", c)])], True))
            prep_seq(items)
        if stage >= 1 and not tbase:
            ffn(0, NEXT, [(0, 352), (352, 352), (704, 352)], 9)
        bar = phase_barrier()

        MC = int(os.environ.get('KMIXCUT', 8))

        def mixer(half):
            gc0 = 4 * half
            if MC < 8 and half == 1 and not os.environ.get('KHALF1'):
                return
            rr = [0]
            pool_banks = [0, 1, 2, 3, 4, 5]

            def nb():
                b = pool_banks[rr[0] % 6]
                rr[0] += 1
                return b

            def hmk(s, e):
                ks = []
                blocks = [(0, 16)] + [(16 + 128 * i, 16 + 128 * (i + 1)) for i in range(4)] + [(528, 544)]
                for bi, (a, b_) in enumerate(blocks):
                    if a < e and b_ > s:
                        ks += [(("hm", bi), q_) for q_ in range(4)]
                return ks

            def win_unit(slot, t0, tn, bank):
                for kc in range(16):
                    first, last = kc == 0, kc == 15
                    B.op("pe", lambda e, kc=kc, first=first, last=last:
                         e.matmul(psb(bank, tn), ring[:, slot, kc * 128:(kc + 1) * 128],
                                  hTm[:, kc, t0:t0 + tn], start=first, stop=last),
                         reads=([("G", slot)] + hmk(t0, t0 + tn)) if first else [],
                         writes=[("ps", bank)] if first else [], signal=last)
                B._commit(B.last["pe"], [("G", slot)] + hmk(t0, t0 + tn), [("ps", bank)])

            if half == 0:
                load_gain(1)
                items = [(8, [(lambda k0: hTm[:, k0:k0 + 4, 0:16], 0, 16, [("hm", 0)]),
                              (lambda k0: stash[:, k0:k0 + 4, 16:32], 16, 32, ["stashR"])], True)]
                for c in range(4):
                    dests = [(lambda k0, c=c: hTm[:, k0:k0 + 4, 16 + 128 * c:16 + 128 * (c + 1)], 0, 128, [("hm", 1 + c)])]
                    if c == 3:
                        dests.append((lambda k0: stash[:, k0:k0 + 4, 0:16], 112, 128, ["stashL"]))
                    items.append((c, dests, True))
                items.append((4, [(lambda k0: hTm[:, k0:k0 + 4, 528:544], 0, 16, [("hm", 5)])], True))
                prep_seq(items)
            else:
                for hk in range(2):
                    B.op("dve", lambda e, hk=hk: e.tensor_copy(out=hTm[:, hk * 8:hk * 8 + 8, 0:16],
                                                               in_=stash[:, hk * 8:hk * 8 + 8, 0:16]),
                         reads=[("stashL", 2 * hk), ("stashL", 2 * hk + 1)], writes=[(("hm", 0), 2 * hk), (("hm", 0), 2 * hk + 1)])
                    B.op("dve", lambda e, hk=hk: e.tensor_copy(out=hTm[:, hk * 8:hk * 8 + 8, 528:544],
                                                               in_=stash[:, hk * 8:hk * 8 + 8, 16:32]),
                         reads=[("stashR", 2 * hk), ("stashR", 2 * hk + 1)], writes=[(("hm", 5), 2 * hk), (("hm", 5), 2 * hk + 1)])
                prep_seq([(c, [(lambda k0, c=c: hTm[:, k0:k0 + 4, 16 + 128 * (c - 4):16 + 128 * (c - 3)], 0, 128,
                                [("hm", 1 + c - 4)])], c != 4) for c in range(4, 8)])

            if MC <= 1:
                return
            ET = [(0, 272), (272, 272)]
            sgi = [0]
            dgi = [0]
            conv_tok = {}

            def glu_unit(cc):
                s2 = load_tile(T_WIN + 16 + cc)
                s1 = load_tile(T_WIN + 8 + cc)
                for (t0, tn) in ET:
                    b2, b1 = nb(), nb()
                    win_unit(s2, t0, tn, b2)
                    win_unit(s1, t0, tn, b1)
                    si = sgi[0] % 2
                    sgi[0] += 1
                    B.op("act", lambda e, si=si, b2=b2, tn=tn: e.activation(out=sgm[:, si, 0:tn], in_=psb(b2, tn), func=AF.Sigmoid),
                         reads=[("ps", b2)], writes=[("sgm", si)])
                    B.op("dve", lambda e, si=si, b1=b1, t0=t0, tn=tn: e.tensor_tensor(
                        out=glu[:, cc, t0:t0 + tn], in0=psb(b1, tn), in1=sgm[:, si, 0:tn], op=ALU.mult),
                        reads=[("ps", b1), ("sgm", si)], writes=[("glu", cc, t0)])

            def conv_unit(cc):
                bank = nb()
                gkeys = [("glu", cc, 0), ("glu", cc, 272)]
                for (j0, j1) in ((0, 8), (8, 16), (16, 24), (24, 31)):
                    dbuf = dgi[0] % 2
                    dgi[0] += 1
                    for j in range(j0, j1):
                        col = C_DWW + cc * 31 + j
                        di_ = dbuf * 8 + (j - j0)
                        B.op("dve", lambda e, di_=di_, col=col: e.tensor_scalar(
                            out=diag[:, di_, :], in0=identb[:, :], scalar1=cv[:, col:col + 1], scalar2=None, op0=ALU.mult),
                            reads=["identb", "cv"] if j == j0 else [], writes=[("diag", dbuf)] if j == j0 else [],
                            signal=(j == j1 - 1))
                    B._commit(B.last["dve"], ["identb", "cv"], [("diag", dbuf)])
                    for j in range(j0, j1):
                        di_ = dbuf * 8 + (j - j0)
                        B.op("pe", lambda e, di_=di_, j=j, bank=bank: e.matmul(
                            psb(bank), diag[:, di_, :], glu[:, cc, 1 + j:513 + j], start=(j == 0), stop=(j == 30)),
                            reads=([("diag", dbuf)] + (gkeys if j == 0 else [])) if j == j0 else [],
                            writes=[("ps", bank)] if j == 0 else [], signal=(j == j1 - 1))
                    B._commit(B.last["pe"], [("diag", dbuf)], [])
                B._commit(B.last["pe"], gkeys, [("ps", bank)])
                B.op("act", lambda e, bank=bank: e.activation(out=convb[:, cc, :], in_=psb(bank), func=AF.Identity,
                                                               bias=cv[:, C_DWB + cc:C_DWB + cc + 1], scale=1.0),
                     reads=[("ps", bank), "cv"], writes=[("conv", cc)])
                B.op("act", lambda e: e.activation(out=sqb[:, 0, :], in_=convb[:, cc, :], func=AF.Square),
                     reads=[("conv", cc)], writes=["sq"])
                B.op("act", lambda e: e.copy(out=hlb[:, 0, :], in_=convb[:, cc, :]), reads=[("conv", cc)], writes=[("hl", 0)])
                B.op("dve", lambda e: e.tensor_tensor(out=hlb[:, 1, :], in0=convb[:, cc, :], in1=hlb[:, 0, :], op=ALU.subtract),
                     reads=[("conv", cc), ("hl", 0)], writes=[("hl", 1)])
                B.op("act", lambda e: e.copy(out=hlb[:, 2, :], in_=sqb[:, 0, :]), reads=["sq"], writes=[("hl", 2)])
                B.op("dve", lambda e: e.tensor_tensor(out=hlb[:, 3, :], in0=sqb[:, 0, :], in1=hlb[:, 2, :], op=ALU.subtract),
                     reads=["sq", ("hl", 2)], writes=[("hl", 3)])

            def stat_unit(cc):
                for i in range(4):
                    bank = 6 + i // 2
                    B.op("pe", lambda e, i=i, bank=bank: e.matmul(psb(bank), ones[:, :], hlb[:, i, :],
                                                                  start=(cc == 0 and i % 2 == 0), stop=(cc == 7 and i % 2 == 1)),
                         reads=[("hl", i), "ones"], writes=[("ps", bank)] if (cc == 0 and i % 2 == 0) else [])
                if cc == 7:
                    B._commit(B.last["pe"], [], [("ps", 6), ("ps", 7)])

            glu_unit(0)
            for cc in range(1, 8):
                glu_unit(cc)
                if MC <= 2:
                    continue
                if cc >= 2:
                    stat_unit(cc - 2)
                conv_unit(cc - 1)
            if MC <= 2:
                return
            stat_unit(6)
            conv_unit(7)
            if MC <= 3:
                stat_unit(7)
                return

            def pool_chunk(cc):
                su = load_tile(T_WIN + cc)
                ub = cc % 2
                for (t0, tn) in ET:
                    b = nb()
                    win_unit(su, t0, tn, b)
                    B.op("act", lambda e, b=b, t0=t0, tn=tn: e.copy(out=ubuf[:, ub, t0:t0 + tn], in_=psb(b, tn)),
                         reads=[("ps", b)], writes=[("ub", ub, t0)])
                if cc == 0:
                    stat_unit(7)
                gi = cc // 2
                w = WINDOWS[gi]
                steps = int(np.log2(w))
                cur = ubuf[:, ub, :]
                curk = [("ub", ub, 0), ("ub", ub, 272)]
                L = HE
                for k in range(steps):
                    sh = 1 << k
                    dst = ptb[:, k % 2, :]
                    B.op("dve", lambda e, cur=cur, dst=dst, L=L, sh=sh: e.tensor_tensor(
                        out=dst[:, 0:L - sh], in0=cur[:, 0:L - sh], in1=cur[:, sh:L], op=ALU.add),
                        reads=curk, writes=[("pt", k % 2)])
                    cur, curk, L = dst, [("pt", k % 2)], L - sh
                o0 = 16 - w // 2
                B.op("dve", lambda e, cur=cur, o0=o0, w=w: e.scalar_tensor_tensor(
                    out=pooled[:, cc, :], in0=cur[:, o0:o0 + 512], scalar=1.0 / w, in1=ubuf[:, ub, 16:528],
                    op0=ALU.mult, op1=ALU.subtract),
                    reads=curk + [("ub", ub, 0), ("ub", ub, 272)], writes=[("pooled", cc)])
                if half == 0:
                    a0, r0 = 0, gi * 16
                else:
                    a0, r0 = 504, gi * 16 + 8
                B.op("dve", lambda e, cur=cur, o0=o0, a0=a0, r0=r0: e.tensor_tensor(
                    out=tmp8[:, 0:8], in0=cur[:, o0 + a0:o0 + a0 + 8], in1=rc[:, r0:r0 + 8], op=ALU.mult),
                    reads=curk + ["rc"], writes=["tmp8"])
                B.op("dve", lambda e, a0=a0: e.tensor_tensor(
                    out=pooled[:, cc, a0:a0 + 8], in0=tmp8[:, 0:8], in1=ubuf[:, ub, 16 + a0:16 + a0 + 8], op=ALU.subtract),
                    reads=["tmp8", ("ub", ub, 0), ("ub", ub, 272)], writes=[("pooled", cc)])
            def grp_unit(gi):
                if True:
                    sg_slot = load_tile(T_PGRP)
                    for oc in range(2):
                        b = nb()
                        for kc in range(2):
                            B.op("pe", lambda e, kc=kc, oc=oc, b=b: e.matmul(
                                psb(b), ring[:, sg_slot, gi * 512 + kc * 256 + oc * 128: gi * 512 + kc * 256 + oc * 128 + 128],
                                pooled[:, 2 * gi + kc, :], start=(kc == 0), stop=(kc == 1)),
                                reads=[("G", sg_slot), ("pooled", 2 * gi), ("pooled", 2 * gi + 1)] if kc == 0 else [],
                                writes=[("ps", b)] if kc == 0 else [], signal=(kc == 1))
                        B._commit(B.last["pe"], [("G", sg_slot), ("pooled", 2 * gi), ("pooled", 2 * gi + 1)], [("ps", b)])
                        mc = 2 * gi + oc
                        B.op("dve", lambda e, b=b, mc=mc: e.tensor_scalar(
                            out=mixedT[:, mc, :], in0=psb(b), scalar1=cv[:, C_PS + mc:C_PS + mc + 1], scalar2=None, op0=ALU.mult),
                            reads=[("ps", b), "cv"], writes=[("mixed", mc)])

            for cc_ in range(8):
                pool_chunk(cc_)
                if cc_ >= 3 and cc_ % 2 == 1:
                    grp_unit((cc_ - 3) // 2)
            grp_unit(3)

            if MC <= 4:
                return
            B.op("dve", lambda e: e.tensor_scalar(out=lnm[:, 0, :], in0=psb(6), scalar1=1.0 / 1024, scalar2=None, op0=ALU.mult),
                 reads=[("ps", 6)], writes=["mean"])
            B.op("dve", lambda e: e.tensor_tensor(out=lnm[:, 1, :], in0=lnm[:, 0, :], in1=lnm[:, 0, :], op=ALU.mult),
                 reads=["mean"], writes=["msq"])
            B.op("dve", lambda e: e.scalar_tensor_tensor(out=lnm[:, 1, :], in0=psb(7), scalar=1.0 / 1024, in1=lnm[:, 1, :],
                                                         op0=ALU.mult, op1=ALU.subtract),
                 reads=[("ps", 7), "msq"], writes=["msq"])
            B.op("dve", lambda e: e.tensor_scalar(out=lnm[:, 1, :], in0=lnm[:, 1, :], scalar1=EPS, scalar2=None, op0=ALU.add),
                 reads=["msq"], writes=["msq"])
            B.op("act", lambda e: e.activation(out=lnm[:, 1, :], in_=lnm[:, 1, :], func=AF.Sqrt),
                 reads=["msq"], writes=["msq"])
            B.op("dve", lambda e: e.reciprocal(out=lnm[:, 2, :], in_=lnm[:, 1, :]),
                 reads=["msq"], writes=["lrstd"])
            for cc in range(8):
                li = cc % 2
                B.op("dve", lambda e, li=li, cc=cc: e.tensor_tensor(out=ltb[:, li, :], in0=convb[:, cc, :], in1=lnm[:, 0, :], op=ALU.subtract),
                     reads=[("conv", cc), "mean"], writes=[("lt", li)])
                B.op("dve", lambda e, li=li: e.tensor_tensor(out=ltb[:, li, :], in0=ltb[:, li, :], in1=lnm[:, 2, :], op=ALU.mult),
                     reads=[("lt", li), "lrstd"], writes=[("lt", li)])
                B.op("act", lambda e, li=li, cc=cc: e.activation(out=ybf[:, cc, :], in_=ltb[:, li, :], func=AF.Silu,
                                                                 bias=cv[:, C_LNB + cc:C_LNB + cc + 1],
                                                                 scale=cv[:, C_LNG + cc:C_LNG + cc + 1]),
                     reads=[("lt", li), "cv"],
                     writes=[("y", cc)] + ([("glu", c_, t_) for c_ in range(8) for t_ in (0, 272)] if cc == 0 else []))

            if MC <= 5:
                return
            OWN0 = 16
            ykeys = [("y", c_) for c_ in range(8)]
            mkeys = [("mixed", c_) for c_ in range(8)]
            ckeys = [("conv", c_) for c_ in range(8)]
            gi_ = [0]

            def sig_unit(tile, col_bias):
                sl_ = load_tile(tile)
                bk = nb()
                g = gi_[0] % 2
                gi_[0] += 1
                win_unit(sl_, OWN0, 512, bk)
                B.op("act", lambda e: e.activation(out=sgm[:, g, :], in_=psb(bk), func=AF.Sigmoid,
                                                   bias=cv[:, col_bias:col_bias + 1], scale=1.0),
                     reads=[("ps", bk), "cv"], writes=[("sgm", g)])
                return g

            def proj_units(tile, src, skeys, dp):
                sl_ = load_tile(tile)
                bks = [nb(), nb()]
                for o in range(2):
                    for kc in range(8):
                        B.op("pe", lambda e, kc=kc, o=o: e.matmul(
                            psb(bks[o]), ring[:, sl_, o * 1024 + kc * 128:o * 1024 + kc * 128 + 128],
                            src[:, kc, :], start=(kc == 0), stop=(kc == 7)),
                            reads=([("G", sl_)] + skeys) if kc == 0 else [], writes=[("ps", bks[o])] if kc == 0 else [],
                            signal=(kc == 7))
                    B._commit(B.last["pe"], [("G", sl_)] + skeys, [("ps", bks[o])])
                return bks

            def gate_a(dc, bk, g):
                B.op("dve", lambda e: e.tensor_tensor(out=gt[:, g, :], in0=psb(bk), in1=sgm[:, g, :], op=ALU.mult),
                     reads=[("ps", bk), ("sgm", g)], writes=[("gt", g)])

            def gate_b(dc, bk, g, ga_):
                B.op("dve", lambda e: e.scalar_tensor_tensor(
                    out=gt[:, 2 + g, :], in0=psb(bk), scalar=cv[:, C_BP + dc:C_BP + dc + 1], in1=sgm[:, g, :],
                    op0=ALU.add, op1=ALU.mult),
                    reads=[("ps", bk), ("sgm", g), "cv"], writes=[("gt", 2 + g)])
                B.op("dve", lambda e: e.tensor_tensor(out=mbf[:, dc, :], in0=gt[:, ga_, :], in1=gt[:, 2 + g, :], op=ALU.add),
                     reads=[("gt", ga_), ("gt", 2 + g)], writes=[("m", dc)] + (ckeys if dc == 0 else []))

            for dp in range(8):
                dc0, dc1 = 2 * dp, 2 * dp + 1
                g = sig_unit(T_WIN + 24 + dc0, C_BG + dc0)
                abk = proj_units(T_PPROJ + dp, mixedT, mkeys, dp)
                gate_a(dc0, abk[0], g)
                ga0 = g
                g = sig_unit(T_WIN + 40 + dc0, C_BG + 16 + dc0)
                bbk = proj_units(T_CPROJ + dp, ybf, ykeys, dp)
                gate_b(dc0, bbk[0], g, ga0)
                g = sig_unit(T_WIN + 24 + dc1, C_BG + dc1)
                gate_a(dc1, abk[1], g)
                ga1 = g
                g = sig_unit(T_WIN + 40 + dc1, C_BG + 16 + dc1)
                gate_b(dc1, bbk[1], g, ga1)

            if MC <= 6:
                return
            mk = [("m", d_) for d_ in range(16)]
            for dt in range(4):
                bks = [0, 1, 2, 3] if dt % 2 == 0 else [4, 5, 6, 7]
                bkeys = [("ps", b_) for b_ in bks]
                for kq in range(4):
                    sl_ = load_tile(T_WOUT + dt * 4 + kq)
                    n = 0
                    for c in range(4):
                        for k4 in range(4):
                            kc = kq * 4 + k4
                            B.op("pe", lambda e, kc=kc, k4=k4, c=c, sl_=sl_, bks=bks: e.matmul(
                                psb(bks[c]), mbf[:, kc, c * 128:(c + 1) * 128],
                                ring[:, sl_, k4 * 512:(k4 + 1) * 512], start=(kc == 0), stop=(kc == 15)),
                                reads=([("G", sl_)] + mk) if n == 0 else [],
                                writes=bkeys if (n == 0 and kq == 0) else [], signal=(n == 15))
                            n += 1
                    B._commit(B.last["pe"], [("G", sl_)] + mk, bkeys if kq == 3 else [])
                for c in range(4):
                    gc = gc0 + c
                    B.op("dve", lambda e, c=c, gc=gc, dt=dt, bks=bks: e.tensor_tensor(
                        out=xres[:, gc, dt * 512:(dt + 1) * 512], in0=psb(bks[c]), in1=xres[:, gc, dt * 512:(dt + 1) * 512], op=ALU.add),
                        reads=[("ps", bks[c])], writes=[("x", gc)])

        if stage >= 2:
            mixer(0)
            bar = phase_barrier()
            mixer(1)
            bar = phase_barrier()

        if stage >= 3:
            ffn_dextra.extend(bar)
            load_gain(2)
            prep_seq([(c, [(lambda k0, c=c: hT[:, k0:k0 + 4, c * 128:(c + 1) * 128], 0, 128, [("hT", c)])], True)
                      for c in range(8)])
            ffn(1, NOWN, [(0, 512), (512, 512)], 8)
            bar = phase_barrier()

        load_gain(3)
        for c in range(8):
            rstd_chunk(c)
            B.op("dve", lambda e, c=c: e.scalar_tensor_tensor(out=xres[:, c, :], in0=xres[:, c, :], scalar=rsb[:, c:c + 1],
                                                              in1=gbc[:, :], op0=ALU.mult, op1=ALU.mult),
                 reads=[("x", c), ("rs", c), "gbc"], writes=[("x", c)])
            B.dma("sp", lambda e, c=c: e.dma_start(out=out[c * 128:(c + 1) * 128, :], in_=xres[:, c, :]),
                  s_out, 16 * (c + 1), reads=[("x", c)])
        B.wait_only("sp", [(s_out, 16 * 8)])

        with nc.Block() as block:
            @block.sync
            def _(e):
                B.replay("sp", e)

            @block.gpsimd
            def _(e):
                B.replay("pool", e)

            @block.tensor
            def _(e):
                B.replay("pe", e)

            @block.scalar
            def _(e):
                B.replay("act", e)

            @block.vector
            def _(e):
                B.replay("dve", e)
    return nc


def _kmajor(W):
    K, O = W.shape
    return W.reshape(K // 128, 128, O // 128, 128).transpose(2, 1, 0, 3).reshape(O // 128, 128, K)


def build_weight_stream(inp):
    wst = np.empty((NT, 128, 2048), np.float32)
    for f, pre in enumerate(("ffn1", "ffn2")):
        v = wst[f * 132:(f + 1) * 132].reshape(NJ, 3, 128, 2048)
        v[:, 0] = _kmajor(np.asarray(inp[pre + "_w_gate"][0]))
        v[:, 1] = _kmajor(np.asarray(inp[pre + "_w_up"][0]))
        v[:, 2] = np.asarray(inp[pre + "_w_down"][0]).reshape(NJ, 128, 2048)
    wst[T_WIN:T_WIN + 56] = _kmajor(np.asarray(inp["w_in"][0]))
    wg = np.asarray(inp["pool_w_group"][0])
    wst[T_PGRP] = wg.reshape(4, 2, 128, 256).transpose(2, 0, 1, 3).reshape(128, 2048)
    pp = _kmajor(np.asarray(inp["pool_w_proj"][0]))
    wst[T_PPROJ:T_PPROJ + 8] = pp.reshape(8, 2, 128, 1024).transpose(0, 2, 1, 3).reshape(8, 128, 2048)
    cp = _kmajor(np.asarray(inp["conv_w_proj"][0]))
    wst[T_CPROJ:T_CPROJ + 8] = cp.reshape(8, 2, 128, 1024).transpose(0, 2, 1, 3).reshape(8, 128, 2048)
    wo = np.asarray(inp["w_out"][0])
    wst[T_WOUT:T_WOUT + 16] = wo.reshape(4, 4, 128, 4, 512).transpose(3, 0, 2, 1, 4).reshape(16, 128, 2048)
    return wst


def _col(v):
    v = np.asarray(v, np.float32).reshape(-1)
    return v.reshape(-1, 128).T


def build_inputs(inp):
    x = np.asarray(inp["x"], np.float32)
    Bsz, T, _ = x.shape
    wst = build_weight_stream(inp)
    gains = np.stack([np.asarray(inp["ffn1_norm"][0]), np.asarray(inp["mix_norm"][0]),
                      np.asarray(inp["ffn2_norm"][0]), np.asarray(inp["final_norm"])]).astype(np.float32)
    cvec = np.zeros((128, NCV), np.float32)
    cvec[:, C_BG:C_BG + 32] = _col(inp["b_gate"][0])
    cvec[:, C_PS:C_PS + 8] = _col(inp["pool_scale"][0])
    cvec[:, C_DWB:C_DWB + 8] = _col(inp["conv_dw_b"][0])
    cvec[:, C_LNG:C_LNG + 8] = _col(inp["conv_ln_g"][0])
    cvec[:, C_LNB:C_LNB + 8] = _col(inp["conv_ln_b"][0])
    cvec[:, C_BP:C_BP + 16] = _col(inp["conv_b_proj"][0])
    dww = np.asarray(inp["conv_dw_w"][0], np.float32)
    cvec[:, C_DWW:] = dww.reshape(31, 8, 128).transpose(2, 1, 0).reshape(128, 248)
    ident = np.eye(128, dtype=np.float32)
    maps = []
    for core in range(8):
        b, q = core // 4, core % 4
        t0 = q * NOWN
        xin = np.zeros((NEXT, D), np.float32)
        xin[0:NOWN] = x[b, t0:t0 + NOWN]
        if t0 - HALO >= 0:
            xin[NOWN:NOWN + HALO] = x[b, t0 - HALO:t0]
        if t0 + NOWN + HALO <= T:
            xin[NOWN + HALO:] = x[b, t0 + NOWN:t0 + NOWN + HALO]
        rcv = np.zeros((4, 16), np.float32)
        for wi, w in enumerate(WINDOWS):
            for i in range(16):
                g = t0 + (i if i < 8 else NOWN - 16 + i)
                lo = max(g - w // 2, 0)
                hi = min(g + (w - w // 2), T)
                rcv[wi, i] = 1.0 / float(hi - lo)
        rcin = np.broadcast_to(rcv.reshape(1, 64), (128, 64)).copy()
        maps.append({"xin": xin, "wst": wst, "gains": gains, "cvec": cvec, "rcin": rcin, "idin": ident})
    return maps


_NC_CACHE = {}


def kernel(**inputs):
    maps = build_inputs(inputs)
    if "nc" not in _NC_CACHE:
        _NC_CACHE["nc"] = build_program()
    nc = _NC_CACHE["nc"]
    res = run_bass_kernel_spmd(nc, maps, core_ids=list(range(8)))
    x = np.asarray(inputs["x"])
    outp = np.empty(x.shape, np.float32)
    for core in range(8):
        b, q = core // 4, core % 4
        outp[b, q * NOWN:(q + 1) * NOWN] = res.results[core]["out"]
    return outp
```
